# Optimizing a Trainium2 kernel written in Bass

```python
import jax, jax.numpy as jnp
from jax import lax
import numpy as np

D_MODEL = 2048
BATCH = 2
SEQ = 8192
DEPTH = 4
DEC_BATCH = 16
DEC_SEQ = 16
PAST_LEN = 1024

CHUNK = 64
N_LEFT_CHUNKS = 8
WINDOW = N_LEFT_CHUNKS * CHUNK
BAND = (N_LEFT_CHUNKS + 1) * CHUNK
N_HEADS = 16
HEAD_DIM = D_MODEL // N_HEADS
MAX_REL = 128
N_REL = 2 * MAX_REL + 1
D_FF = -(-8 * D_MODEL // (3 * 256)) * 256
Q_BLOCK = 128
N_A_LAYERS = (DEPTH + 1) // 2
N_B_LAYERS = DEPTH // 2
FORGET_BIAS_INIT = 3.0
EPS = 1e-6
SCALE = HEAD_DIM ** -0.5
NEG = -1e30

kernel_name = "hybrid_chunkband_fox_stream_step"


def rms_norm(x, g):
    xf = x.astype(jnp.float32)
    y = xf * lax.rsqrt(jnp.mean(xf * xf, axis=-1, keepdims=True) + EPS)
    return (y * g.astype(jnp.float32)).astype(x.dtype)


def qkv_heads(h, w_qkv, g_q, g_k):
    b, s, _ = h.shape
    qkv = (h @ w_qkv).reshape(b, s, 3, N_HEADS, HEAD_DIM)
    return rms_norm(qkv[:, :, 0], g_q), rms_norm(qkv[:, :, 1], g_k), qkv[:, :, 2]


def masked_softmax(logits, mask):
    return jax.nn.softmax(jnp.where(mask, logits, NEG), axis=-1)


def rel_bias(table, q_pos, k_pos):
    d = jnp.clip(q_pos[:, None] - k_pos[None, :], -MAX_REL, MAX_REL) + MAX_REL
    return table[:, d].astype(jnp.float32)


def band_attention_prompt(q, k, v, table):
    b, s, H, hd = q.shape
    nc = s // CHUNK
    qc = q.reshape(b, nc, CHUNK, H, hd)
    pad = jnp.zeros((b, WINDOW, H, hd), k.dtype)
    kp = jnp.concatenate([pad, k], axis=1).reshape(b, nc + N_LEFT_CHUNKS, CHUNK, H, hd)
    vp = jnp.concatenate([pad, v], axis=1).reshape(b, nc + N_LEFT_CHUNKS, CHUNK, H, hd)
    idx = jnp.arange(nc)[:, None] + jnp.arange(N_LEFT_CHUNKS + 1)[None, :]
    kb = kp[:, idx].reshape(b, nc, BAND, H, hd)
    vb = vp[:, idx].reshape(b, nc, BAND, H, hd)
    bias = rel_bias(table, WINDOW + jnp.arange(CHUNK), jnp.arange(BAND))
    logits = jnp.einsum('bcqhd,bckhd->bchqk', qc, kb,
                        preferred_element_type=jnp.float32) * SCALE + bias[None, None]
    key_valid = jnp.repeat(idx >= N_LEFT_CHUNKS, CHUNK, axis=1)
    p = masked_softmax(logits, key_valid[None, :, None, None, :])
    out = jnp.einsum('bchqk,bckhd->bcqhd', p.astype(v.dtype), vb)
    return out.reshape(b, s, H * hd)


def band_attention_sample(q, k_new, v_new, k_cache, v_cache, table):
    L = k_cache.shape[1]
    b, n, H, hd = q.shape
    k = jnp.concatenate([k_cache.astype(k_new.dtype), k_new], axis=1)
    v = jnp.concatenate([v_cache.astype(v_new.dtype), v_new], axis=1)
    bias = rel_bias(table, L + jnp.arange(n), jnp.arange(L + n))
    logits = jnp.einsum('bqhd,bkhd->bhqk', q, k,
                        preferred_element_type=jnp.float32) * SCALE + bias[None]
    p = jax.nn.softmax(logits, axis=-1)
    out = jnp.einsum('bhqk,bkhd->bqhd', p.astype(v.dtype), v)
    return out.reshape(b, n, H * hd)


def log_forget(h, w_f, b_f):
    return jax.nn.log_sigmoid((h @ w_f).astype(jnp.float32) + b_f.astype(jnp.float32))


def forgetting_attention_prompt(q, k, v, logf):
    b, s, H, hd = q.shape
    nb = s // Q_BLOCK
    c = jnp.cumsum(logf, axis=1).transpose(0, 2, 1)
    qb = q.reshape(b, nb, Q_BLOCK, H, hd).transpose(1, 0, 2, 3, 4)
    cb = c.reshape(b, H, nb, Q_BLOCK).transpose(2, 0, 1, 3)
    starts = jnp.arange(nb) * Q_BLOCK
    k_pos = jnp.arange(s)

    def block(args):
        qi, ci, start = args
        logits = jnp.einsum('bqhd,bkhd->bhqk', qi, k, preferred_element_type=jnp.float32) * SCALE
        logits = logits + ci[..., None] - c[:, :, None, :]
        mask = k_pos[None, :] <= (start + jnp.arange(Q_BLOCK))[:, None]
        p = masked_softmax(logits, mask)
        return jnp.einsum('bhqk,bkhd->bqhd', p.astype(v.dtype), v)

    out = lax.map(block, (qb, cb, starts))
    return out.transpose(1, 0, 2, 3, 4).reshape(b, s, H * hd)


def forgetting_attention_sample(q, k_new, v_new, logf_new, k_cache, v_cache, logf_cache):
    L = k_cache.shape[1]
    b, n, H, hd = q.shape
    k = jnp.concatenate([k_cache.astype(k_new.dtype), k_new], axis=1)
    v = jnp.concatenate([v_cache.astype(v_new.dtype), v_new], axis=1)
    c = jnp.cumsum(jnp.concatenate([logf_cache.astype(jnp.float32), logf_new], axis=1),
                   axis=1).transpose(0, 2, 1)
    logits = jnp.einsum('bqhd,bkhd->bhqk', q, k, preferred_element_type=jnp.float32) * SCALE
    logits = logits + c[:, :, L:, None] - c[:, :, None, :]
    mask = jnp.arange(L + n)[None, :] <= (L + jnp.arange(n))[:, None]
    p = masked_softmax(logits, mask)
    out = jnp.einsum('bhqk,bkhd->bqhd', p.astype(v.dtype), v)
    return out.reshape(b, n, H * hd)


def swiglu(h, w_gate, w_up, w_down):
    return (jax.nn.silu(h @ w_gate) * (h @ w_up)) @ w_down


def setup_inputs(seed: int = 0) -> dict:
    key = jax.random.key(seed)
    ks = jax.random.split(key, 20)
    D, H, hd, F = D_MODEL, N_HEADS, HEAD_DIM, D_FF
    la = min(WINDOW, PAST_LEN)
    f32 = jnp.float32
    nrm = jax.random.normal
    return {
        'x_prompt': nrm(ks[0], (BATCH, SEQ, D), f32),
        'x_sample': nrm(ks[1], (DEC_BATCH, DEC_SEQ, D), f32),
        'cache_a_k': nrm(ks[2], (N_A_LAYERS, DEC_BATCH, la, H, hd), f32),
        'cache_a_v': nrm(ks[3], (N_A_LAYERS, DEC_BATCH, la, H, hd), f32),
        'cache_b_k': nrm(ks[4], (N_B_LAYERS, DEC_BATCH, PAST_LEN, H, hd), f32),
        'cache_b_v': nrm(ks[5], (N_B_LAYERS, DEC_BATCH, PAST_LEN, H, hd), f32),
        'cache_b_logf': jax.nn.log_sigmoid(FORGET_BIAS_INIT + nrm(ks[6], (N_B_LAYERS, DEC_BATCH, PAST_LEN, H), f32)),
        'g_attn': 1.0 + 0.05 * nrm(ks[7], (DEPTH, D), f32),
        'w_qkv': nrm(ks[8], (DEPTH, D, 3 * D), f32) * D ** -0.5,
        'g_q': 1.0 + 0.05 * nrm(ks[9], (DEPTH, hd), f32),
        'g_k': 1.0 + 0.05 * nrm(ks[10], (DEPTH, hd), f32),
        'w_o': nrm(ks[11], (DEPTH, D, D), f32) * D ** -0.5,
        'rel_table': 0.5 * nrm(ks[12], (N_A_LAYERS, H, N_REL), f32),
        'w_f': nrm(ks[13], (N_B_LAYERS, D, H), f32) * D ** -0.5,
        'b_f': FORGET_BIAS_INIT + 0.5 * nrm(ks[14], (N_B_LAYERS, H), f32),
        'g_ffn': 1.0 + 0.05 * nrm(ks[15], (DEPTH, D), f32),
        'w_gate': nrm(ks[16], (DEPTH, D, F), f32) * D ** -0.5,
        'w_up': nrm(ks[17], (DEPTH, D, F), f32) * D ** -0.5,
        'w_down': nrm(ks[18], (DEPTH, F, D), f32) * F ** -0.5,
    }


def reference(x_prompt, x_sample, cache_a_k, cache_a_v, cache_b_k, cache_b_v, cache_b_logf,
              g_attn, w_qkv, g_q, g_k, w_o, rel_table, w_f, b_f, g_ffn, w_gate, w_up, w_down):
    yp, ys = x_prompt, x_sample
    keep = min(WINDOW, x_prompt.shape[1])
    a_kp, a_vp, b_kp, b_vp, b_fp = [], [], [], [], []
    a_ks, a_vs, b_ks, b_vs, b_fs = [], [], [], [], []
    for layer in range(DEPTH):
        hp = rms_norm(yp, g_attn[layer])
        hs = rms_norm(ys, g_attn[layer])
        qp, kp, vp = qkv_heads(hp, w_qkv[layer], g_q[layer], g_k[layer])
        qs, ks, vs = qkv_heads(hs, w_qkv[layer], g_q[layer], g_k[layer])
        if layer % 2 == 0:
            ia = layer // 2
            mp = band_attention_prompt(qp, kp, vp, rel_table[ia])
            ms = band_attention_sample(qs, ks, vs, cache_a_k[ia], cache_a_v[ia], rel_table[ia])
            a_kp.append(kp[:, -keep:])
            a_vp.append(vp[:, -keep:])
            a_ks.append(ks)
            a_vs.append(vs)
        else:
            ib = layer // 2
            lfp = log_forget(hp, w_f[ib], b_f[ib])
            lfs = log_forget(hs, w_f[ib], b_f[ib])
            mp = forgetting_attention_prompt(qp, kp, vp, lfp)
            ms = forgetting_attention_sample(qs, ks, vs, lfs, cache_b_k[ib], cache_b_v[ib], cache_b_logf[ib])
            b_kp.append(kp)
            b_vp.append(vp)
            b_fp.append(lfp)
            b_ks.append(ks)
            b_vs.append(vs)
            b_fs.append(lfs)
        yp = yp + mp @ w_o[layer]
        ys = ys + ms @ w_o[layer]
        yp = yp + swiglu(rms_norm(yp, g_ffn[layer]), w_gate[layer], w_up[layer], w_down[layer])
        ys = ys + swiglu(rms_norm(ys, g_ffn[layer]), w_gate[layer], w_up[layer], w_down[layer])
    return (yp, ys,
            jnp.stack(a_kp), jnp.stack(a_vp), jnp.stack(b_kp), jnp.stack(b_vp), jnp.stack(b_fp),
            jnp.stack(a_ks), jnp.stack(a_vs), jnp.stack(b_ks), jnp.stack(b_vs), jnp.stack(b_fs))
```

```python
import contextlib
import os
import numpy as np
import ml_dtypes
import concourse.bass as bass
import concourse.mybir as mybir
from concourse.bass_utils import run_bass_kernel_spmd

F32 = mybir.dt.float32
BF16 = mybir.dt.bfloat16
AF = mybir.ActivationFunctionType
ALU = mybir.AluOpType
AX = mybir.AxisListType

D = 2048
H = 16
HD = 128
FF = 5632
NFC = FF // 128
DEPTH = 4
NTOK = 2080
NPR = 2048
EPS = 1e-6
SCALE = HD ** -0.5
NEGM = -30000.0
LA = 512
LB = 1024


class Buf:
    __slots__ = ("name", "w", "r", "const", "excl")

    def __init__(self, name, const=False, excl=False):
        self.name = name
        self.w = None
        self.r = {}
        self.const = const
        self.excl = excl


class _Eng:
    def __init__(self, name, is_pe=False):
        self.name = name
        self.is_pe = is_pe
        self.sem = None
        self.count = 0
        self.ops = []
        self.known = {}
        self.dsems = []
        self.dnext = 0


class Sched:
    def __init__(self, nc):
        self.nc = nc
        self.eng = {k: _Eng(k, k == "pe") for k in ("pe", "act", "dve", "pool", "sp")}
        self.n_dma = {"sp": 14, "pool": 10}
        self.sems = {}
        self._ctx = []
        self.n_ops = 0

    def _newsem(self, key):
        cm = self.nc.semaphore(key)
        h = cm.__enter__()
        self._ctx.append(cm)
        self.sems[key] = h
        return h

    def setup(self):
        for k in ("pe", "act", "dve", "pool"):
            self.eng[k].sem = k
            self._newsem(k)
        for q, n in self.n_dma.items():
            for i in range(n):
                key = f"d_{q}_{i}"
                self._newsem(key)
                self.eng[q].dsems.append([key, 0])

    def close(self):
        for cm in reversed(self._ctx):
            cm.__exit__(None, None, None)

    def _need(self, e, tok, waits):
        if tok is None:
            return
        key, val = tok
        if e.known.get(key, 0) >= val:
            return
        if e.is_pe and key == "pe":
            return
        e.known[key] = val
        waits.append((key, val))

    def _deps(self, e, reads, writes):
        waits = []
        for b in reads:
            self._need(e, b.w, waits)
            if b.excl:
                for k, v in b.r.items():
                    if k != e.sem:
                        self._need(e, (k, v), waits)
        for b in writes:
            self._need(e, b.w, waits)
            for k, v in b.r.items():
                self._need(e, (k, v), waits)
        return waits

    def _commit(self, tok, reads, writes):
        k, v = tok
        for b in reads:
            if b.const:
                continue
            if b.r.get(k, 0) < v:
                b.r[k] = v
        for b in writes:
            b.w = tok
            b.r = {}

    def op(self, eng, fn, reads=(), writes=(), inc=True):
        e = self.eng[eng]
        waits = self._deps(e, reads, writes)
        if inc:
            e.count += 1
            tok = (e.sem, e.count)
        else:
            tok = (e.sem, e.count + 1)
        self._commit(tok, reads, writes)
        e.ops.append((waits, fn, (e.sem, 1) if inc else None))
        self.n_ops += 1
        return tok

    def dma(self, q, fn, reads=(), writes=()):
        e = self.eng[q]
        waits = self._deps(e, reads, writes)
        slot = e.dsems[e.dnext]
        e.dnext = (e.dnext + 1) % len(e.dsems)
        key, n = slot
        if n > 0:
            self._need(e, (key, 16 * n), waits)
        slot[1] = n + 1
        tok = (key, 16 * (n + 1))
        self._commit(tok, reads, writes)
        e.ops.append((waits, fn, (key, 16)))
        self.n_ops += 1
        return tok

    def coll(self, fn, reads=(), writes=()):
        e = self.eng["pool"]
        waits = self._deps(e, reads, writes)
        key = f"cc_{len(self.sems)}"
        self._newsem(key)
        tok = (key, 1)
        self._commit(tok, reads, writes)
        e.ops.append((waits, fn, (key, None)))
        self.n_ops += 1
        return tok

    def all_tokens(self):
        toks = []
        for k in ("pe", "act", "dve", "pool"):
            e = self.eng[k]
            if e.count > 0:
                toks.append((e.sem, e.count))
        for q in self.n_dma:
            for key, n in self.eng[q].dsems:
                if n > 0:
                    toks.append((key, 16 * n))
        return toks

    def barrier(self, engines=("pe", "act", "dve", "pool", "sp")):
        toks = self.all_tokens()
        for k in engines:
            e = self.eng[k]
            waits = []
            for t in toks:
                self._need(e, t, waits)
            if waits:
                e.ops.append((waits, None, None))

    def run(self):
        nc = self.nc
        sems = self.sems

        def replay(e, h):
            for waits, fn, inc in e.ops:
                for key, val in waits:
                    h.wait_ge(sems[key], val)
                if fn is None:
                    continue
                ins = fn(h)
                if inc is not None:
                    key, v = inc
                    if v is None:
                        ins.then_inc(sems[key])
                    else:
                        ins.then_inc(sems[key], v)

        with nc.Block() as block:
            @block.sync
            def _(h):
                replay(self.eng["sp"], h)

            @block.tensor
            def _(h):
                replay(self.eng["pe"], h)

            @block.scalar
            def _(h):
                replay(self.eng["act"], h)

            @block.vector
            def _(h):
                replay(self.eng["dve"], h)

            @block.gpsimd
            def _(h):
                replay(self.eng["pool"], h)


TILES = [(t, 128 * t, 128) for t in range(16)] + [(16, 2048, 32)]
GROUPS = [[0, 1, 2, 3], [4, 5, 6, 7], [8, 9, 10, 11], [12, 13, 14, 15, 16]]
ALL8 = [list(range(8))]
GRP4 = [[0, 1, 2, 3], [4, 5, 6, 7]]


def build(NL=DEPTH, stop_after=None):
    nc = bass.Bass("TRN2", target_bir_lowering=False)
    NB = max(NL // 2, 1)

    def din(name, shape, dt=F32):
        return nc.dram_tensor(name, list(shape), dt, kind="ExternalInput").ap()

    def dout(name, shape, dt=F32):
        return nc.dram_tensor(name, list(shape), dt, kind="ExternalOutput").ap()

    def dscr(name, shape, dt):
        return nc.dram_tensor(name, list(shape), dt).ap()

    xin = din("xin", [NTOK, D])
    cak = din("cak", [2, 2, LA, D]); cav = din("cav", [2, 2, LA, D])
    cbk = din("cbk", [2, 2, LB, D]); cbv = din("cbv", [2, 2, LB, D])
    cblf = din("cblf", [2, 2, LB, H])
    g_attn = din("g_attn", [DEPTH, D]); g_ffn = din("g_ffn", [DEPTH, D])
    g_q = din("g_q", [DEPTH, HD]); g_k = din("g_k", [DEPTH, HD])
    b_f = din("b_f", [2, H])
    wqkv_s = din("wqkv_s", [DEPTH, D // 8, 3 * D]); wo_s = din("wo_s", [DEPTH, D // 8, D])
    wg_s = din("wg_s", [DEPTH, D // 8, FF]); wu_s = din("wu_s", [DEPTH, D // 8, FF])
    wd_s = din("wd_s", [DEPTH, FF // 8, D]); wf_s = din("wf_s", [2, D // 8, H])
    biasA = din("biasA", [2, H, 128, 640]); maskA = din("maskA", [128, 640])
    biasS = din("biasS", [2, H, 128, 80])
    maskD = din("maskD", [128, 4 * 512], BF16)
    maskS = din("maskS", [16, 16])
    tabs = din("tabs", [128, 24]); onehot = din("onehot", [16, 8])
    y = dout("y", [NTOK, D])
    ko = dout("ko", [DEPTH, NTOK, D]); vo = dout("vo", [DEPTH, NTOK, D])
    lfo = dout("lfo", [2, NTOK, H])
    xs = [dscr(f"xs{i}", [NTOK, D], F32) for i in range(3)]
    mk = lambda nm, shp, dt, n: [dscr(f"{nm}{l}", shp, dt) for l in range(n)]
    wqkv_c = mk("wqkv_c", [D // 8, 3 * D], BF16, NL); wqkv_b = mk("wqkv_b", [D, 3 * D], BF16, NL)
    wo_c = mk("wo_c", [D // 8, D], BF16, NL); wo_b = mk("wo_b", [D, D], BF16, NL)
    wg_c = mk("wg_c", [D // 8, FF], BF16, NL); wg_b = mk("wg_b", [D, FF], BF16, NL)
    wu_c = mk("wu_c", [D // 8, FF], BF16, NL); wu_b = mk("wu_b", [D, FF], BF16, NL)
    wd_c = mk("wd_c", [FF // 8, D], BF16, NL); wd_b = mk("wd_b", [FF, D], BF16, NL)
    wf_c = mk("wf_c", [D // 8, H], BF16, NB); wf_b = mk("wf_b", [D, H], BF16, NB)
    qtl = mk("qtl", [H * HD, NTOK], BF16, NL)
    ktl = mk("ktl", [H * HD, NPR], BF16, NL)
    kts = mk("kts", [H * HD, 32], BF16, NL)
    vl = mk("vl", [NTOK, D], BF16, NL)
    ktg = mk("ktg", [8 * H * HD, NPR], BF16, NL)
    vg = mk("vg", [8 * NPR, D], BF16, NL)
    lfTl = mk("lfTl", [H, NPR], F32, NB); lfTs = mk("lfTs", [H, 32], F32, NB)
    lfTg = mk("lfTg", [8 * H, NPR], F32, NB)
    csn = mk("csn", [3, H, 2, 8 * 1024], BF16, NB)
    csp = mk("csp", [3, H, 2, 1024], BF16, NB)
    cssn = mk("cssn", [2, 3, H, LB + 16], BF16, NB)
    cssp = mk("cssp", [2, 3, H, 16], BF16, NB)

    S = Sched(nc)
    S.setup()
    es = contextlib.ExitStack()

    def sb(name, shape, dt):
        return es.enter_context(nc.sbuf_tensor(name, list(shape), dt))

    def ps(name, shape, dt):
        return es.enter_context(nc.psum_tensor(name, list(shape), dt))

    big = sb("big", [128, 16 * NTOK], BF16)
    big2 = sb("big2", [128, 32768], BF16)
    big3 = sb("big3", [128, 28416], BF16)
    st = sb("st", [128, 4, 16], F32)
    epst = sb("epst", [128, 1], F32)
    ident = sb("ident", [128, 128], BF16)
    identf = sb("identf", [128, 128], F32)
    ones = sb("ones", [128, 128], BF16)
    gq = sb("gq", [128, HD], F32)
    gk = sb("gk", [128, HD], F32)
    mS = sb("mS", [16, 16], F32)
    tabt = sb("tabt", [128, 24], F32)
    oht = sb("oht", [16, 8], F32)
    esmp = sb("esmp", [128, 80], F32)
    bfb = sb("bfb", [128, H], F32)
    wfs = sb("wfs", [128, 16, H], BF16)
    lft = sb("lft", [128, 4, H], F32)
    qts = sb("qts", [128, H, 32], BF16)
    ktsb = sb("ktsb", [128, H, 32], BF16)
    augsb = sb("augsb", [6, 16], BF16)
    fb = [ps(f"fb{i}", [128, 512], F32) for i in range(6)]
    tb = [ps(f"tb{i}", [128, 1024], BF16) for i in range(2)]
    B_fb = [Buf(f"fb{i}", excl=True) for i in range(6)]
    B_tb = [Buf(f"tb{i}", excl=True) for i in range(2)]

    def v3(boff, n, dt):
        if dt == BF16:
            return big3[:, boff // 2:boff // 2 + n]
        return big3[:, boff // 2:boff // 2 + 2 * n].bitcast(F32)

    gb = v3(0, 2048, F32)
    hb = v3(8192, 4096, BF16).rearrange("p (s d) -> p s d", s=2)
    sq = v3(16384, 1024, F32).rearrange("p (s d) -> p s d", s=2)
    qn = v3(20480, 1024, F32).rearrange("p (s d) -> p s d", s=2)
    kf = v3(24576, 1024, F32).rearrange("p (s d) -> p s d", s=2)
    qbt = v3(28672, 1024, BF16).rearrange("p (s d) -> p s d", s=2)
    qst = v3(30720, 4096, BF16).rearrange("p (s d) -> p s d", s=2)
    vbt = v3(38912, 1024, BF16).rearrange("p (s d) -> p s d", s=2)
    lfT = v3(40960, NTOK, F32)
    x5 = v3(49280, 1024, F32).rearrange("p (s d) -> p s d", s=2)
    qsm = v3(53376, 128, BF16)
    ptt = v3(0, 1536, BF16).rearrange("p (s d) -> p s d", s=3)
    sadd = v3(3072, 1024, F32).rearrange("p (s d) -> p s d", s=2)
    rden = v3(7168, 512, F32)
    ecat = v3(9216 + 4096, 640, F32)
    eraw = v3(9216 + 4096 + 2560, 640, F32)
    mA = v3(9216 + 4096 + 5120, 640, F32)
    mD = v3(9216, 2048, BF16)
    hbS = v3(27136, 2048, BF16)
    vnew = v3(31232, 2048, BF16)
    knew = v3(35328, 256, BF16).rearrange("p (h t) -> p h t", h=H)
    stg = v3(35840, 2048, F32)
    cws = v3(44032, 2112, F32).rearrange("p (s d) -> p s d", s=2)
    cbs = v3(52480, 1056, BF16)
    augsa = v3(54592, 1056, BF16)

    hT = big[:, :].rearrange("p (k t) -> p k t", k=16)
    actT = big[:, 0:NFC * 544].rearrange("p (k t) -> p k t", k=NFC)
    h2T = big[:, NFC * 544:(NFC + 16) * 544].rearrange("p (k t) -> p k t", k=16)
    cbf = big[0:16, 0:4 * NPR]
    wr = [big2[:, i * 8192:(i + 1) * 8192] for i in range(3)]
    xg = big2[:, 24576:32768].bitcast(F32).rearrange("p (s d) -> p s d", s=2)
    KTb = big2[:, 0:8192]
    Vb = big2[:, 8192:16384]
    QTh = [big2[:, 16384 + i * NTOK:16384 + (i + 1) * NTOK] for i in range(2)]
    auga = big2[:, 20544:28736]
    KTo = big2[:, 20544:22592]
    Vo = big2[:, 22592:24640]
    augb = big2[:, 28736:30784]
    KCs = big2[:, 0:16384].rearrange("p (h k) -> p h k", h=H)
    VCs = big2[:, 16384:32768].rearrange("p (c d) -> p c d", c=8)
    cwork = big2[0:16, 0:16384].bitcast(F32)
    cw2 = big2[0:16, 16384:32768].bitcast(F32)

    B = {}

    def bf(name, **kw):
        if name not in B:
            B[name] = Buf(name, **kw)
        return B[name]

    Bbig = [bf(f"big_t{t}") for t in range(17)]
    Bwr = [bf(f"wr{i}") for i in range(3)]
    Bxg = [bf(f"xg{i}") for i in range(2)]
    cnt = {"fb": 0, "tb": 0, "wr": 0, "q": 0, "pt": 0, "sb": 0}

    def next_fb(lo=0, hi=4):
        i = lo + cnt["fb"] % (hi - lo)
        cnt["fb"] += 1
        return i

    def next_tb():
        i = cnt["tb"] % 2
        cnt["tb"] += 1
        return i

    S.op("pool", lambda e: e.memset(identf[:], 0.0), writes=[bf("identf")])
    S.op("pool", lambda e: e.affine_select(out=identf[:], in_=identf[:], pattern=[[-1, 128]], compare_op=ALU.not_equal,
                                           fill=1.0, base=0, channel_multiplier=1),
         reads=[bf("identf")], writes=[bf("identf")])
    S.op("dve", lambda e: e.tensor_copy(out=ident[:], in_=identf[:]), reads=[bf("identf")], writes=[bf("ident")])
    S.op("dve", lambda e: e.memset(epst[:], EPS), writes=[bf("epst")])
    S.op("dve", lambda e: e.memset(ones[:], 1.0), writes=[bf("ones")])
    S.op("dve", lambda e: e.memset(augsb[:], 1.0), writes=[bf("augsb")])
    S.dma("sp", lambda e: e.dma_start(out=mS[:], in_=maskS), writes=[bf("mS")])
    S.dma("sp", lambda e: e.dma_start(out=tabt[:], in_=tabs), writes=[bf("tabt")])
    S.dma("sp", lambda e: e.dma_start(out=oht[:], in_=onehot), writes=[bf("oht")])
    for nm in ("ident", "identf", "epst", "ones", "mS", "tabt", "oht"):
        B[nm].const = True

    def prep_weights(l):
        items = [(wqkv_s[l], wqkv_c[l], wqkv_b[l], f"wqkv{l}")]
        if l % 2 == 1:
            items.append((wf_s[l // 2], wf_c[l // 2], wf_b[l // 2], f"wf{l}"))
        items += [(wo_s[l], wo_c[l], wo_b[l], f"wo{l}"), (wg_s[l], wg_c[l], wg_b[l], f"wg{l}"),
                  (wu_s[l], wu_c[l], wu_b[l], f"wu{l}"), (wd_s[l], wd_c[l], wd_b[l], f"wd{l}")]
        for src, cst, dst, name in items:
            S.dma("pool", lambda e, src=src, cst=cst: e.dma_start(out=cst, in_=src), writes=[bf(name + "_c")])
        for src, cst, dst, name in items:
            S.coll(lambda e, cst=cst, dst=dst: e.collective_compute("AllGather", ALU.bypass, replica_groups=ALL8,
                                                                    ins=[cst.opt()], outs=[dst.opt()]),
                   reads=[bf(name + "_c")], writes=[bf(name)])

    def load_w(wdram, name, rows0, nk, col0, ncol, slot, off=0):
        view = wr[slot][:, off:off + nk * ncol].rearrange("p (k n) -> p k n", k=nk)
        src = wdram[rows0:rows0 + nk * 128, col0:col0 + ncol].rearrange("(k p) n -> p k n", p=128)
        S.dma("sp", lambda e: e.dma_start(out=view, in_=src), reads=[bf(name)], writes=[Bwr[slot]])
        return view

    def norm_tiles(src, gain_row, tiles, dstT, dst_col0, Bdst):
        S.dma("sp", lambda e: e.dma_start(out=gb, in_=gain_row.partition_broadcast(128)), writes=[bf("gb")])
        ld = {}

        def load(n):
            tt, r0, P = tiles[n]
            xi = n % 2
            ld[n] = xi
            S.dma("sp", lambda e: e.dma_start(out=xg[0:P, xi, :], in_=src[r0:r0 + P, :]), reads=[bf("xsrc_any")], writes=[Bxg[xi]])
        load(0)
        for n, (tt, r0, P) in enumerate(tiles):
            if n + 1 < len(tiles):
                load(n + 1)
            xi = ld[n]
            hs = n % 2
            Bh = bf(f"hb{hs}")
            Bs = bf(f"st{hs}")
            S.op("act", lambda e, xi=xi, hs=hs, P=P: e.activation(out=hb[0:P, hs, :], in_=xg[0:P, xi, :], func=AF.Square,
                                                                accum_out=st[0:P, hs, 0:1]),
                 reads=[Bxg[xi]], writes=[Bh, Bs])
            S.op("act", lambda e, hs=hs, P=P: e.activation(out=st[0:P, hs, 1:2], in_=st[0:P, hs, 0:1], func=AF.Sqrt,
                                                          bias=epst[0:P, :], scale=1.0 / D),
                 reads=[Bs, B["epst"]], writes=[Bs])
            S.op("dve", lambda e, hs=hs, P=P: e.reciprocal(out=st[0:P, hs, 2:3], in_=st[0:P, hs, 1:2]), reads=[Bs], writes=[Bs])
            S.op("dve", lambda e, xi=xi, hs=hs, P=P: e.scalar_tensor_tensor(out=hb[0:P, hs, :], in0=xg[0:P, xi, :],
                                                                          scalar=st[0:P, hs, 2:3], in1=gb[0:P, :],
                                                                          op0=ALU.mult, op1=ALU.mult),
                 reads=[Bxg[xi], Bs, bf("gb")], writes=[Bh])
            c0 = dst_col0(n)
            for half in range(2):
                ti = next_tb()
                for i in range(8):
                    kc = half * 8 + i
                    S.op("pe", lambda e, ti=ti, i=i, kc=kc, hs=hs, P=P: e.transpose(out=tb[ti][:, i * 128:i * 128 + P],
                                                                                   in_=hb[0:P, hs, kc * 128:(kc + 1) * 128],
                                                                                   identity=ident[0:P, 0:P]),
                         reads=[Bh, B["ident"]], writes=[B_tb[ti]])
                src_v = tb[ti][:, :].rearrange("p (k t) -> p k t", k=8)[:, :, 0:P]
                dst_v = dstT[:, half * 8:(half + 1) * 8, c0:c0 + P]
                if half == 0:
                    S.op("act", lambda e, src_v=src_v, dst_v=dst_v: e.copy(out=dst_v, in_=src_v), reads=[B_tb[ti]], writes=[Bdst(n)])
                else:
                    S.op("dve", lambda e, src_v=src_v, dst_v=dst_v: e.tensor_copy(out=dst_v, in_=src_v), reads=[B_tb[ti]],
                         writes=[Bdst(n)])

    def attend(qT, nq, tiles_list, out_dst, Bq, Bout):
        oi = 4 + cnt["q"] % 2
        di = 2 + cnt["q"] % 2
        cnt["q"] += 1
        nt = len(tiles_list)
        for ti_, T in enumerate(tiles_list):
            nk, c0, n = T["nk"], T["c0"], T["n"]
            si = cnt["sb"] % 2
            cnt["sb"] += 1
            sbank = fb[si]
            aug = T.get("aug")
            S.op("pe", lambda e, T=T, sbank=sbank, nk=nk, c0=c0, n=n, aug=aug: e.matmul(
                sbank[0:nk, 0:n], lhsT=T["kT"], rhs=qT[:, c0:c0 + n], start=True, stop=(aug is None)),
                 reads=[T["Bk"], Bq], writes=[B_fb[si]], inc=(aug is None))
            if aug is not None:
                S.op("pe", lambda e, sbank=sbank, nk=nk, n=n, aug=aug: e.matmul(sbank[0:nk, 0:n], lhsT=aug[0], rhs=aug[1],
                                                                                 start=False, stop=True),
                     reads=list(T["Bm"]), writes=[B_fb[si]], inc=True)
            pi = cnt["pt"] % 3
            cnt["pt"] += 1
            Bp = bf(f"ptt{pi}")
            pt = ptt[0:nk, pi, 0:n]
            bias = T.get("bias")
            bias = 0.0 if bias is None else bias
            if T.get("addm") is not None:
                ai = ti_ % 2
                Ba = bf(f"sadd{ai}")
                if T.get("flag") is not None:
                    S.op("dve", lambda e, T=T, ai=ai, nk=nk, n=n, sbank=sbank: e.scalar_tensor_tensor(
                        out=sadd[0:nk, ai, 0:n], in0=T["addm"], scalar=T["flag"], in1=sbank[0:nk, 0:n], op0=ALU.mult, op1=ALU.add),
                         reads=[B_fb[si]] + list(T["Bm"]), writes=[Ba])
                else:
                    S.op("dve", lambda e, T=T, ai=ai, nk=nk, n=n, sbank=sbank: e.tensor_tensor(
                        out=sadd[0:nk, ai, 0:n], in0=sbank[0:nk, 0:n], in1=T["addm"], op=ALU.add),
                         reads=[B_fb[si]] + list(T["Bm"]), writes=[Ba])
                S.op("act", lambda e, ai=ai, nk=nk, n=n, pt=pt, bias=bias: e.activation(out=pt, in_=sadd[0:nk, ai, 0:n], func=AF.Exp, bias=bias),
                     reads=[Ba, B["tabt"]], writes=[Bp])
            else:
                S.op("act", lambda e, nk=nk, n=n, pt=pt, sbank=sbank, bias=bias: e.activation(out=pt, in_=sbank[0:nk, 0:n], func=AF.Exp, bias=bias),
                     reads=[B_fb[si], B["tabt"]], writes=[Bp])
            if T.get("mulE") is not None:
                S.op("dve", lambda e, T=T, pt=pt: e.tensor_tensor(out=pt, in0=pt, in1=T["mulE"], op=ALU.mult),
                     reads=[Bp] + list(T["Bm"]), writes=[Bp])
            first = ti_ == 0
            last = ti_ == nt - 1
            S.op("pe", lambda e, T=T, pt=pt, c0=c0, n=n, first=first, last=last: e.matmul(
                fb[oi][:, c0:c0 + n], lhsT=T["v"], rhs=pt, start=first, stop=last),
                 reads=[T["Bv"], Bp], writes=[B_fb[oi]], inc=False)
            S.op("pe", lambda e, nk=nk, pt=pt, c0=c0, n=n, first=first, last=last: e.matmul(
                fb[di][:, c0:c0 + n], lhsT=ones[0:nk, :], rhs=pt, start=first, stop=last),
                 reads=[Bp, B["ones"]], writes=[B_fb[di]], inc=True)
        S.op("dve", lambda e: e.reciprocal(out=rden[:, 0:nq], in_=fb[di][:, 0:nq]), reads=[B_fb[di]], writes=[bf("rden")])
        S.op("dve", lambda e: e.tensor_tensor(out=out_dst, in0=fb[oi][:, 0:nq], in1=rden[:, 0:nq], op=ALU.mult),
             reads=[B_fb[oi], bf("rden")], writes=list(Bout))

    def splits(srcap, n, sign, dst_of, name, dstname, cb):
        for k3 in range(3):
            S.op("dve", lambda e: e.tensor_scalar(out=cb[:, 0:n], in0=srcap, scalar1=sign, scalar2=None, op0=ALU.mult),
                 reads=[bf(name)], writes=[bf("cbf")])
            S.dma("sp", lambda e, k3=k3: e.dma_start(out=dst_of(k3), in_=cb[:, 0:n]), reads=[bf("cbf")], writes=[bf(dstname)])
            if k3 < 2:
                S.op("dve", lambda e: e.scalar_tensor_tensor(out=srcap, in0=cb[:, 0:n], scalar=-sign, in1=srcap,
                                                            op0=ALU.mult, op1=ALU.add),
                     reads=[bf("cbf"), bf(name)], writes=[bf(name)])

    def layer(L, xsrc, xmid, xdst):
        isB = (L % 2 == 1)
        il = L // 2
        S.dma("sp", lambda e: e.dma_start(out=gq[:], in_=g_q[L:L + 1, :].partition_broadcast(128)), writes=[bf("gq")])
        S.dma("sp", lambda e: e.dma_start(out=gk[:], in_=g_k[L:L + 1, :].partition_broadcast(128)), writes=[bf("gk")])
        S.op("dve", lambda e: e.tensor_scalar(out=gq[:], in0=gq[:], scalar1=SCALE, scalar2=None, op0=ALU.mult),
             reads=[bf("gq")], writes=[bf("gq")])
        if isB:
            S.dma("sp", lambda e: e.dma_start(out=bfb[:], in_=b_f[il:il + 1, :].partition_broadcast(128)), writes=[bf("bfb")])
            S.dma("sp", lambda e: e.dma_start(out=wfs[:], in_=wf_b[il].rearrange("(k p) n -> p k n", p=128)),
                  reads=[bf(f"wf{L}")], writes=[bf("wfs")])

        norm_tiles(xsrc, g_attn[L:L + 1, :], TILES, hT, lambda n: TILES[n][1], lambda n: Bbig[n])

        if isB:
            for n, (tt, r0, P) in enumerate(TILES):
                fi = next_fb()
                for kc in range(16):
                    S.op("pe", lambda e, fi=fi, kc=kc, r0=r0, P=P: e.matmul(fb[fi][0:P, 0:H], lhsT=hT[:, kc, r0:r0 + P], rhs=wfs[:, kc, :],
                                                                           start=(kc == 0), stop=(kc == 15)),
                         reads=[Bbig[n], bf("wfs")], writes=[B_fb[fi]], inc=(kc == 15))
                ls = n % 4
                Bl = bf(f"lft{ls}")
                S.op("dve", lambda e, fi=fi, ls=ls, P=P: e.tensor_tensor(out=lft[0:P, ls, :], in0=fb[fi][0:P, 0:H], in1=bfb[0:P, :], op=ALU.add),
                     reads=[B_fb[fi], bf("bfb")], writes=[Bl])
                S.op("act", lambda e, ls=ls, P=P: e.activation(out=lft[0:P, ls, :], in_=lft[0:P, ls, :], func=AF.Exp, scale=-1.0),
                     reads=[Bl], writes=[Bl])
                S.op("act", lambda e, ls=ls, P=P: e.activation(out=lft[0:P, ls, :], in_=lft[0:P, ls, :], func=AF.Ln, bias=1.0),
                     reads=[Bl], writes=[Bl])
                S.op("dve", lambda e, ls=ls, P=P: e.tensor_scalar(out=lft[0:P, ls, :], in0=lft[0:P, ls, :], scalar1=-1.0, scalar2=None,
                                                                op0=ALU.mult),
                     reads=[Bl], writes=[Bl])
                S.dma("sp", lambda e, ls=ls, r0=r0, P=P: e.dma_start(out=lfo[il, r0:r0 + P, :], in_=lft[0:P, ls, :]), reads=[Bl])
                fj = next_fb()
                S.op("pe", lambda e, fj=fj, ls=ls, P=P: e.transpose(out=fb[fj][0:H, 0:P], in_=lft[0:P, ls, :], identity=identf[0:P, 0:P]),
                     reads=[Bl, B["identf"]], writes=[B_fb[fj]])
                S.op("act", lambda e, fj=fj, r0=r0, P=P: e.copy(out=lfT[0:H, r0:r0 + P], in_=fb[fj][0:H, 0:P]), reads=[B_fb[fj]],
                     writes=[bf("lfT")])
            S.dma("sp", lambda e: e.dma_start(out=lfTl[il], in_=lfT[0:H, 0:NPR]), reads=[bf("lfT")], writes=[bf(f"lfTl{il}")])
            S.dma("sp", lambda e: e.dma_start(out=lfTs[il], in_=lfT[0:H, NPR:NTOK]), reads=[bf("lfT")], writes=[bf(f"lfTs{il}")])
            S.coll(lambda e: e.collective_compute("AllGather", ALU.bypass, replica_groups=ALL8,
                                                  ins=[lfTl[il].opt()], outs=[lfTg[il].opt()]),
                   reads=[bf(f"lfTl{il}")], writes=[bf(f"lfTg{il}")])

        wv_next = load_w(wqkv_b[L], f"wqkv{L}", 0, 16, 0, 512, 0)
        for ng in range(12):
            slot = ng % 3
            wv, Bw = wv_next, Bwr[slot]
            if ng + 1 < 12:
                wv_next = load_w(wqkv_b[L], f"wqkv{L}", 0, 16, (ng + 1) * 512, 512, (ng + 1) % 3)
            kind = ng // 4
            hb0 = (ng % 4) * 4
            for gi, grp in enumerate(GROUPS):
                qs = (ng * 4 + gi) % 2
                Bqst = bf(f"qst{qs}")
                for n in grp:
                    tt, r0, P = TILES[n]
                    fi = next_fb()
                    for kc in range(16):
                        S.op("pe", lambda e, fi=fi, kc=kc, r0=r0, P=P, wv=wv: e.matmul(fb[fi][0:P, :], lhsT=hT[:, kc, r0:r0 + P],
                                                                                   rhs=wv[:, kc, :], start=(kc == 0), stop=(kc == 15)),
                             reads=[Bbig[n], Bw], writes=[B_fb[fi]], inc=(kc == 15))
                    s2 = n % 2
                    if kind < 2:
                        Bsq, Bst, Bqn, Bkf, Bqb = bf(f"sq{s2}"), bf(f"stq{s2}"), bf(f"qn{s2}"), bf(f"kf{s2}"), bf(f"qbt{s2}")
                        S.op("act", lambda e, fi=fi, s2=s2, P=P: e.activation(out=sq[0:P, s2, :], in_=fb[fi][0:P, :], func=AF.Square),
                             reads=[B_fb[fi]], writes=[Bsq])
                        S.op("dve", lambda e, s2=s2, P=P: e.tensor_reduce(out=st[0:P, 2 + s2, 0:4],
                                                                        in_=sq[0:P, s2, :].rearrange("p (h d) -> p h d", h=4),
                                                                        axis=AX.X, op=ALU.add),
                             reads=[Bsq], writes=[Bst])
                        S.op("act", lambda e, s2=s2, P=P: e.activation(out=st[0:P, 2 + s2, 4:8], in_=st[0:P, 2 + s2, 0:4], func=AF.Sqrt,
                                                                     bias=epst[0:P, :], scale=1.0 / HD),
                             reads=[Bst, B["epst"]], writes=[Bst])
                        S.op("dve", lambda e, s2=s2, P=P: e.reciprocal(out=st[0:P, 2 + s2, 8:12], in_=st[0:P, 2 + s2, 4:8]),
                             reads=[Bst], writes=[Bst])
                        S.op("dve", lambda e, fi=fi, s2=s2, P=P: e.tensor_tensor(
                            out=qn[0:P, s2, :].rearrange("p (h d) -> p h d", h=4),
                            in0=fb[fi][0:P, :].rearrange("p (h d) -> p h d", h=4),
                            in1=st[0:P, 2 + s2, 8:12].unsqueeze(2).to_broadcast([P, 4, 128]), op=ALU.mult),
                             reads=[B_fb[fi], Bst], writes=[Bqn])
                        gsrc = gq if kind == 0 else gk
                        Bg = bf("gq") if kind == 0 else bf("gk")
                        S.op("dve", lambda e, s2=s2, P=P, gsrc=gsrc: e.tensor_tensor(
                            out=kf[0:P, s2, :].rearrange("p (h d) -> p h d", h=4),
                            in0=qn[0:P, s2, :].rearrange("p (h d) -> p h d", h=4),
                            in1=gsrc[0:P, :].unsqueeze(1).to_broadcast([P, 4, 128]), op=ALU.mult),
                             reads=[Bqn, Bg], writes=[Bkf])
                        if kind == 1:
                            S.dma("sp", lambda e, s2=s2, r0=r0, P=P, ng=ng: e.dma_start(
                                out=ko[L, r0:r0 + P, (ng - 4) * 512:(ng - 3) * 512], in_=kf[0:P, s2, :]), reads=[Bkf])
                        S.op("act", lambda e, s2=s2, P=P: e.copy(out=qbt[0:P, s2, :], in_=kf[0:P, s2, :]), reads=[Bkf], writes=[Bqb])
                        ti = next_tb()
                        for hh in range(4):
                            S.op("pe", lambda e, ti=ti, hh=hh, s2=s2, P=P: e.transpose(out=tb[ti][:, hh * 128:hh * 128 + P],
                                                                                     in_=qbt[0:P, s2, hh * 128:(hh + 1) * 128],
                                                                                     identity=ident[0:P, 0:P]),
                                 reads=[Bqb, B["ident"]], writes=[B_tb[ti]])
                        srcv = tb[ti][:, 0:512].rearrange("p (h t) -> p h t", h=4)[:, :, 0:P]
                        if n < 16:
                            cq = 128 * (n % 4)
                            dstv = qst[:, qs, :].rearrange("p (h t) -> p h t", h=4)[:, :, cq:cq + P]
                            S.op("dve", lambda e, srcv=srcv, dstv=dstv: e.tensor_copy(out=dstv, in_=srcv), reads=[B_tb[ti]], writes=[Bqst])
                        else:
                            dsts = qts if kind == 0 else ktsb
                            dname = "qts" if kind == 0 else "ktsb"
                            dstv = dsts[:, hb0:hb0 + 4, :]
                            S.op("dve", lambda e, srcv=srcv, dstv=dstv: e.tensor_copy(out=dstv, in_=srcv), reads=[B_tb[ti]],
                                 writes=[bf(dname)])
                    else:
                        Bvf, Bvb = bf(f"kf{s2}"), bf(f"vbt{s2}")
                        S.op("act", lambda e, fi=fi, s2=s2, P=P: e.copy(out=kf[0:P, s2, :], in_=fb[fi][0:P, :]), reads=[B_fb[fi]], writes=[Bvf])
                        S.dma("sp", lambda e, s2=s2, r0=r0, P=P, ng=ng: e.dma_start(
                            out=vo[L, r0:r0 + P, (ng - 8) * 512:(ng - 7) * 512], in_=kf[0:P, s2, :]), reads=[Bvf])
                        S.op("dve", lambda e, s2=s2, P=P: e.tensor_copy(out=vbt[0:P, s2, :], in_=kf[0:P, s2, :]), reads=[Bvf], writes=[Bvb])
                        S.dma("sp", lambda e, s2=s2, r0=r0, P=P, ng=ng: e.dma_start(
                            out=vl[L][r0:r0 + P, (ng - 8) * 512:(ng - 7) * 512], in_=vbt[0:P, s2, :]), reads=[Bvb], writes=[bf(f"vl{L}")])
                if kind < 2:
                    c0 = 512 * gi
                    srcq = qst[:, qs, :].rearrange("p (h t) -> p h t", h=4)
                    dstd = qtl[L] if kind == 0 else ktl[L]
                    dname = f"qtl{L}" if kind == 0 else f"ktl{L}"
                    S.dma("sp", lambda e, srcq=srcq, hb0=hb0, c0=c0, dstd=dstd: e.dma_start(
                        out=dstd[hb0 * 128:(hb0 + 4) * 128, c0:c0 + 512].rearrange("(h d) t -> d h t", h=4), in_=srcq),
                          reads=[Bqst], writes=[bf(dname)])
        S.coll(lambda e: e.collective_compute("AllGather", ALU.bypass, replica_groups=ALL8,
                                              ins=[ktl[L].opt()], outs=[ktg[L].opt()]),
               reads=[bf(f"ktl{L}")], writes=[bf(f"ktg{L}")])
        S.coll(lambda e: e.collective_compute("AllGather", ALU.bypass, replica_groups=ALL8,
                                              ins=[vl[L][0:NPR, :].opt()], outs=[vg[L].opt()]),
               reads=[bf(f"vl{L}")], writes=[bf(f"vg{L}")])
        if L + 1 < NL:
            prep_weights(L + 1)
        S.barrier()
        if stop_after == ("qkv", L):
            return False

        if isB:
            for bb in range(2):
                S.dma("sp", lambda e, bb=bb: e.dma_start(out=cwork.rearrange("h (r t) -> h r t", r=8),
                                                        in_=lfTg[il].rearrange("(r h) t -> h r t", r=8)[:, :, 1024 * bb:1024 * bb + 1024]),
                      reads=[bf(f"lfTg{il}")], writes=[bf("cwork")])
                prev = None
                for j in range(2):
                    for r in range(8):
                        a0 = r * 1024 + 512 * j
                        init = 0.0 if prev is None else cw2[:, prev + 511:prev + 512]
                        S.op("dve", lambda e, a0=a0, init=init: e.tensor_tensor_scan(out=cw2[:, a0:a0 + 512], data0=cwork[:, a0:a0 + 512],
                                                                                   data1=cwork[:, a0:a0 + 512], initial=init,
                                                                                   op0=ALU.add, op1=ALU.bypass),
                             reads=[bf("cwork"), bf("cw2")], writes=[bf("cw2")])
                        prev = a0
                S.op("dve", lambda e: e.tensor_scalar(out=cwork[:, 0:1024], in0=cw2[:, 0:1024], scalar1=oht[:, 0:1], scalar2=None, op0=ALU.mult),
                     reads=[bf("cw2"), B["oht"], bf("cwork")], writes=[bf("cwork")])
                for r in range(1, 8):
                    S.op("dve", lambda e, r=r: e.scalar_tensor_tensor(out=cwork[:, 0:1024], in0=cw2[:, r * 1024:(r + 1) * 1024],
                                                                    scalar=oht[:, r:r + 1], in1=cwork[:, 0:1024], op0=ALU.mult, op1=ALU.add),
                         reads=[bf("cw2"), B["oht"], bf("cwork")], writes=[bf("cwork")])
                splits(cwork[:, 0:1024], 1024, 1.0, lambda k3, bb=bb: csp[il][k3, :, bb, :], "cwork", "csp", cbf)
                splits(cw2[:, :], 8192, -1.0, lambda k3, bb=bb: csn[il][k3, :, bb, :], "cw2", "csn", cbf)
            S.barrier()

        if isB:
            S.dma("sp", lambda e: e.dma_start(out=mD, in_=maskD), writes=[bf("mD")])
            S.op("dve", lambda e: e.memset(auga[0:6, :], 1.0), writes=[bf("auga")])
            S.op("dve", lambda e: e.memset(augb[0:6, :], 1.0), writes=[bf("augb")])
        else:
            S.dma("sp", lambda e: e.dma_start(out=mA, in_=maskA), writes=[bf("mA")])
        KT8 = KTb.rearrange("p (r t) -> p r t", r=8)
        V8 = Vb.rearrange("p (r c d) -> p r c d", r=8, d=128)
        Vo3 = Vo.rearrange("p (c d) -> p c d", d=128)
        BK, BV = bf("KT"), bf("V")
        for h in range(H):
            pb = h % 2
            BQ = bf(f"QT{pb}")
            S.dma("sp", lambda e, pb=pb, h=h: e.dma_start(out=QTh[pb][:, 0:NPR], in_=qtl[L][h * 128:(h + 1) * 128, 0:NPR]),
                  reads=[bf(f"qtl{L}")], writes=[BQ])
            if not isB:
                Be = bf("ecat")
                S.dma("sp", lambda e, h=h: e.dma_start(out=eraw, in_=biasA[il, h]), writes=[bf("eraw")])
                S.op("act", lambda e: e.activation(out=eraw, in_=eraw, func=AF.Exp), reads=[bf("eraw")], writes=[bf("eraw")])
                S.op("dve", lambda e: e.tensor_tensor(out=ecat, in0=eraw, in1=mA, op=ALU.mult),
                     reads=[bf("eraw"), bf("mA")], writes=[Be])
                BKo, BVo = bf("KTo"), bf("Vo")
                S.dma("sp", lambda e, h=h: e.dma_start(out=KTo, in_=ktl[L][h * 128:(h + 1) * 128, :]), reads=[bf(f"ktl{L}")], writes=[BKo])
                S.dma("sp", lambda e, h=h: e.dma_start(out=Vo3, in_=vl[L][0:NPR, h * 128:(h + 1) * 128].rearrange("(c k) d -> k c d", k=128)),
                      reads=[bf(f"vl{L}")], writes=[BVo])
            for bb in range(2):
                t0 = 1024 * bb
                S.dma("sp", lambda e, h=h, t0=t0: e.dma_start(
                    out=KT8, in_=ktg[L].rearrange("(r hd) t -> hd r t", r=8)[h * 128:(h + 1) * 128, :, t0:t0 + 1024]),
                      reads=[bf(f"ktg{L}")], writes=[BK])
                for r in range(8):
                    S.dma("sp", lambda e, h=h, r=r, t0=t0: e.dma_start(
                        out=V8[:, r, :, :],
                        in_=vg[L][r * NPR + t0:r * NPR + t0 + 1024, h * 128:(h + 1) * 128].rearrange("(c k) d -> k c d", k=128)),
                          reads=[bf(f"vg{L}")], writes=[BV])
                if isB:
                    Baa, Bab = bf("auga"), bf("augb")
                    S.dma("sp", lambda e, h=h, bb=bb: e.dma_start(out=auga[3:6, :], in_=csn[il][:, h, bb, :]), reads=[bf("csn")], writes=[Baa])
                    S.dma("sp", lambda e, h=h, bb=bb: e.dma_start(out=augb[0:3, 0:1024], in_=csp[il][:, h, bb, :]), reads=[bf("csp")],
                          writes=[Bab])
                for j in range(2):
                    lg = 2 * bb + j
                    qc0 = 512 * lg
                    tl = []
                    if isB:
                        for jp in range(j + 1):
                            for r in range(8):
                                for u in range(4):
                                    kc0 = 512 * jp + 128 * u
                                    T = dict(kT=KT8[:, r, kc0:kc0 + 128], v=V8[:, r, 4 * jp + u, :], nk=128, c0=0, n=512, Bk=BK, Bv=BV,
                                             aug=(auga[0:6, r * 1024 + kc0:r * 1024 + kc0 + 128], augb[0:6, 512 * j:512 * j + 512]),
                                             Bm=[Baa, Bab])
                                    if jp == j:
                                        T["addm"] = mD[:, u * 512:(u + 1) * 512]
                                        T["flag"] = tabt[:, 8 + r:9 + r]
                                        T["bias"] = tabt[:, r:r + 1]
                                        T["Bm"] = [Baa, Bab, bf("mD")]
                                    tl.append(T)
                    else:
                        for kt in range(4):
                            n = 512 - 128 * kt
                            tl.append(dict(kT=KTo[:, qc0 + 128 * kt:qc0 + 128 * kt + 128], v=Vo3[:, 4 * lg + kt, :], nk=128,
                                           c0=128 * kt, n=n, Bk=BKo, Bv=BVo, mulE=ecat[:, 0:n], Bm=[Be]))
                        for r in range(8):
                            jj = j if r < 7 else j - 1
                            if jj < 0:
                                continue
                            for ktp in range(4):
                                n = 128 * (ktp + 1)
                                kc0 = 512 * jj + 128 * ktp
                                tl.append(dict(kT=KT8[:, r, kc0:kc0 + 128], v=V8[:, r, 4 * jj + ktp, :], nk=128, c0=0, n=n, Bk=BK, Bv=BV,
                                               mulE=ecat[:, 128 * (4 - ktp):640], bias=tabt[:, 16 + r:17 + r], Bm=[Be]))
                    attend(QTh[pb][:, qc0:qc0 + 512], 512, tl, hT[:, h, qc0:qc0 + 512], BQ,
                           [Bbig[4 * lg + i] for i in range(4)])
        S.barrier()

        if stop_after == ("pattn", L):
            return False
        nct = 8 if isB else 4
        Lc = nct * 128
        ck = cbk if isB else cak
        cv = cbv if isB else cav
        if isB:
            S.op("dve", lambda e: e.memset(augsa[0:6, :], 1.0), writes=[bf("augsa")])
        for s in range(2):
            for c in range(nct):
                S.dma("sp", lambda e, c=c, s=s: e.dma_start(out=stg, in_=ck[il, s, c * 128:(c + 1) * 128, :]), writes=[bf("stg")])
                S.op("dve", lambda e: e.tensor_copy(out=hbS, in_=stg), reads=[bf("stg")], writes=[bf("hbS")])
                for half in range(2):
                    ti = next_tb()
                    for i in range(8):
                        hh = half * 8 + i
                        S.op("pe", lambda e, ti=ti, i=i, hh=hh: e.transpose(out=tb[ti][:, i * 128:(i + 1) * 128],
                                                                          in_=hbS[:, hh * 128:(hh + 1) * 128], identity=ident[:]),
                             reads=[bf("hbS"), B["ident"]], writes=[B_tb[ti]])
                    S.op("act", lambda e, ti=ti, half=half, c=c: e.copy(out=KCs[:, half * 8:(half + 1) * 8, c * 128:(c + 1) * 128],
                                                                     in_=tb[ti][:, :].rearrange("p (k t) -> p k t", k=8)),
                         reads=[B_tb[ti]], writes=[bf("KCs")])
                S.dma("sp", lambda e, c=c, s=s: e.dma_start(out=stg, in_=cv[il, s, c * 128:(c + 1) * 128, :]), writes=[bf("stg")])
                S.op("act", lambda e, c=c: e.copy(out=VCs[:, c, :], in_=stg), reads=[bf("stg")], writes=[bf("VCs")])
            S.dma("sp", lambda e, s=s: e.dma_start(out=vnew[0:16, :], in_=vl[L][NPR + 16 * s:NPR + 16 * s + 16, :]),
                  reads=[bf(f"vl{L}")], writes=[bf("vnew")])
            if isB:
                for c in range(nct):
                    S.dma("sp", lambda e, c=c, s=s: e.dma_start(out=lft[:, c % 4, :], in_=cblf[il, s, c * 128:(c + 1) * 128, :]),
                          writes=[bf(f"lft{c % 4}")])
                    fj = next_fb()
                    S.op("pe", lambda e, fj=fj, c=c: e.transpose(out=fb[fj][0:H, 0:128], in_=lft[:, c % 4, :], identity=identf[:]),
                         reads=[bf(f"lft{c % 4}"), B["identf"]], writes=[B_fb[fj]])
                    S.op("act", lambda e, fj=fj, c=c: e.copy(out=cws[0:H, 0, c * 128:(c + 1) * 128], in_=fb[fj][0:H, 0:128]),
                         reads=[B_fb[fj]], writes=[bf("cws0")])
                S.dma("sp", lambda e, s=s: e.dma_start(out=cws[0:H, 0, Lc:Lc + 16], in_=lfTs[il][:, 16 * s:16 * s + 16]),
                      reads=[bf(f"lfTs{il}")], writes=[bf("cws0")])
                S.op("dve", lambda e: e.tensor_tensor_scan(out=cws[0:H, 1, 0:Lc + 16], data0=cws[0:H, 0, 0:Lc + 16],
                                                           data1=cws[0:H, 0, 0:Lc + 16], initial=0.0, op0=ALU.add, op1=ALU.bypass),
                     reads=[bf("cws0")], writes=[bf("cws1")])
                S.op("dve", lambda e: e.tensor_copy(out=cws[0:H, 0, 0:16], in_=cws[0:H, 1, Lc:Lc + 16]), reads=[bf("cws1"), bf("cws0")],
                     writes=[bf("cws0")])
                splits(cws[0:H, 0, 0:16], 16, 1.0, lambda k3, s=s: cssp[il][s, k3], "cws0", "cssp", cbs[0:H, :])
                splits(cws[0:H, 1, 0:Lc + 16], Lc + 16, -1.0, lambda k3, s=s: cssn[il][s, k3], "cws1", "cssn", cbs[0:H, :])
            if stop_after == ("sprep", L):
                S.barrier()
                return False
            for h in range(H):
                tl = []
                if isB:
                    S.dma("sp", lambda e, h=h, s=s: e.dma_start(out=augsa[3:6, 0:LB + 16], in_=cssn[il][s, :, h, :]), reads=[bf("cssn")],
                          writes=[bf("augsa")])
                    S.dma("sp", lambda e, h=h, s=s: e.dma_start(out=augsb[0:3, :], in_=cssp[il][s, :, h, :]), reads=[bf("cssp")],
                          writes=[bf("augsb")])
                else:
                    S.dma("sp", lambda e, h=h: e.dma_start(out=esmp[:], in_=biasS[il, h]), writes=[bf("esmp")])
                    S.op("act", lambda e: e.activation(out=esmp[:], in_=esmp[:], func=AF.Exp), reads=[bf("esmp")], writes=[bf("esmp")])
                for c in range(nct):
                    T = dict(kT=KCs[:, h, c * 128:(c + 1) * 128], v=VCs[:, c, h * 128:(h + 1) * 128], nk=128, c0=0, n=16,
                             Bk=bf("KCs"), Bv=bf("VCs"), Bm=[])
                    if isB:
                        T["aug"] = (augsa[0:6, c * 128:(c + 1) * 128], augsb[0:6, :])
                        T["Bm"] = [bf("augsa"), bf("augsb")]
                    else:
                        T["mulE"] = esmp[:, c * 16:(c + 1) * 16]
                        T["Bm"] = [bf("esmp")]
                    tl.append(T)
                T = dict(kT=ktsb[:, h, 16 * s:16 * s + 16], v=vnew[0:16, h * 128:(h + 1) * 128], nk=16, c0=0, n=16, Bk=bf("ktsb"), Bv=bf("vnew"), Bm=[])
                if isB:
                    T["aug"] = (augsa[0:6, Lc:Lc + 16], augsb[0:6, :])
                    T["addm"] = mS[:, :]
                    T["Bm"] = [bf("augsa"), bf("augsb"), B["mS"]]
                else:
                    T["mulE"] = esmp[0:16, 64:80]
                    T["Bm"] = [bf("esmp")]
                tl.append(T)
                attend(qts[:, h, 16 * s:16 * s + 16], 16, tl, hT[:, h, NPR + 16 * s:NPR + 16 * s + 16], bf("qts"), [Bbig[16]])
        S.barrier()
        if stop_after == ("attn", L):
            return False

        wv_next = load_w(wo_b[L], f"wo{L}", 0, 16, 0, 512, 0)
        for ng in range(4):
            slot = ng % 3
            wv, Bw = wv_next, Bwr[slot]
            if ng + 1 < 4:
                wv_next = load_w(wo_b[L], f"wo{L}", 0, 16, (ng + 1) * 512, 512, (ng + 1) % 3)
            for n, (tt, r0, P) in enumerate(TILES):
                fi = next_fb()
                xi = n % 2
                Bx = bf(f"x5_{xi}")
                S.dma("sp", lambda e, xi=xi, r0=r0, P=P, ng=ng: e.dma_start(out=x5[0:P, xi, :], in_=xsrc[r0:r0 + P, ng * 512:(ng + 1) * 512]),
                      reads=[bf("xsrc_any")], writes=[Bx])
                for hh in range(16):
                    S.op("pe", lambda e, fi=fi, hh=hh, r0=r0, P=P, wv=wv: e.matmul(fb[fi][0:P, :], lhsT=hT[:, hh, r0:r0 + P], rhs=wv[:, hh, :],
                                                                               start=(hh == 0), stop=(hh == 15)),
                         reads=[Bbig[n], Bw], writes=[B_fb[fi]], inc=(hh == 15))
                S.op("dve", lambda e, fi=fi, xi=xi, P=P: e.tensor_tensor(out=x5[0:P, xi, :], in0=fb[fi][0:P, :], in1=x5[0:P, xi, :], op=ALU.add),
                     reads=[B_fb[fi], Bx], writes=[Bx])
                S.dma("sp", lambda e, xi=xi, r0=r0, P=P, ng=ng: e.dma_start(out=xmid[r0:r0 + P, ng * 512:(ng + 1) * 512], in_=x5[0:P, xi, :]),
                      reads=[Bx], writes=[bf("xsrc_any")])
        S.barrier()

        for gi, grp in enumerate(GROUPS):
            tiles = [TILES[n] for n in grp]
            T = sum(t[2] for t in tiles)
            col_of = {}
            c = 0
            for k_, t in enumerate(tiles):
                col_of[k_] = c
                c += t[2]
            Bh2 = [bf(f"h2T_{k_}") for k_ in range(5)]
            Bact = [bf(f"actT_{k_}") for k_ in range(5)]
            norm_tiles(xmid, g_ffn[L:L + 1, :], tiles, h2T, lambda n: col_of[n], lambda n: Bh2[n])
            segs = [(0, 512)] + ([(512, 32)] if T > 512 else [])

            def load_gu(fb2, slot):
                a = load_w(wg_b[L], f"wg{L}", 0, 16, fb2 * 256, 256, slot, 0)
                b_ = load_w(wu_b[L], f"wu{L}", 0, 16, fb2 * 256, 256, slot, 4096)
                return a, b_
            nxt = load_gu(0, 0)
            for fb2 in range(22):
                slot = fb2 % 3
                wgv, wuv = nxt
                Bw = Bwr[slot]
                if fb2 + 1 < 22:
                    nxt = load_gu(fb2 + 1, (fb2 + 1) % 3)
                for fc in range(2):
                    f = fb2 * 2 + fc
                    for (s0, sn) in segs:
                        gi_ = next_fb()
                        ui_ = next_fb()
                        rb = Bh2[0:4] if s0 == 0 else [Bh2[4]]
                        for kc in range(16):
                            S.op("pe", lambda e, gi_=gi_, kc=kc, fc=fc, s0=s0, sn=sn, wgv=wgv: e.matmul(
                                fb[gi_][:, 0:sn], lhsT=wgv[:, kc, fc * 128:(fc + 1) * 128], rhs=h2T[:, kc, s0:s0 + sn],
                                start=(kc == 0), stop=(kc == 15)),
                                 reads=rb + [Bw], writes=[B_fb[gi_]], inc=(kc == 15))
                        for kc in range(16):
                            S.op("pe", lambda e, ui_=ui_, kc=kc, fc=fc, s0=s0, sn=sn, wuv=wuv: e.matmul(
                                fb[ui_][:, 0:sn], lhsT=wuv[:, kc, fc * 128:(fc + 1) * 128], rhs=h2T[:, kc, s0:s0 + sn],
                                start=(kc == 0), stop=(kc == 15)),
                                 reads=rb + [Bw], writes=[B_fb[ui_]], inc=(kc == 15))
                        si = (f + (1 if s0 else 0)) % 2
                        Bs_ = bf(f"sq{si}")
                        S.op("act", lambda e, gi_=gi_, si=si, sn=sn: e.activation(out=sq[:, si, 0:sn], in_=fb[gi_][:, 0:sn], func=AF.Silu),
                             reads=[B_fb[gi_]], writes=[Bs_])
                        wb_ = Bact[0:4] if s0 == 0 else [Bact[4]]
                        S.op("dve", lambda e, ui_=ui_, si=si, s0=s0, sn=sn, f=f: e.tensor_tensor(
                            out=actT[:, f, s0:s0 + sn], in0=fb[ui_][:, 0:sn], in1=sq[:, si, 0:sn], op=ALU.mult),
                             reads=[B_fb[ui_], Bs_], writes=wb_)
            blocks = [(ng, piece) for ng in range(4) for piece in range(4)]
            wslot = {}

            def load_d(bi):
                ng, piece = blocks[bi]
                slot = (22 + bi) % 3
                wslot[bi] = (load_w(wd_b[L], f"wd{L}", piece * 11 * 128, 11, ng * 512, 512, slot), Bwr[slot])
            load_d(0)
            banks = None
            for bi, (ng, piece) in enumerate(blocks):
                if bi + 1 < len(blocks):
                    load_d(bi + 1)
                wdv, Bwd = wslot[bi]
                if piece == 0:
                    base = 0 if ng % 2 == 0 else 1
                    banks = [(ng * 5 + k_) % 6 for k_ in range(len(tiles))]
                for k_, (tt, r0, P) in enumerate(tiles):
                    for fcl in range(11):
                        f = piece * 11 + fcl
                        S.op("pe", lambda e, b_=banks[k_], f=f, fcl=fcl, c_=col_of[k_], P=P, wdv=wdv: e.matmul(
                            fb[b_][0:P, :], lhsT=actT[:, f, c_:c_ + P], rhs=wdv[:, fcl, :], start=(f == 0), stop=(f == NFC - 1)),
                             reads=[Bact[k_], Bwd], writes=[B_fb[banks[k_]]], inc=(fcl == 10))
                if piece == 3:
                    for k_, (tt, r0, P) in enumerate(tiles):
                        xi = (ng * 5 + k_) % 2
                        Bx = bf(f"x5_{xi}")
                        S.dma("sp", lambda e, xi=xi, r0=r0, P=P, ng=ng: e.dma_start(out=x5[0:P, xi, :],
                                                                                in_=xmid[r0:r0 + P, ng * 512:(ng + 1) * 512]),
                              reads=[bf("xsrc_any")], writes=[Bx])
                        S.op("dve", lambda e, b_=banks[k_], xi=xi, P=P: e.tensor_tensor(
                            out=x5[0:P, xi, :], in0=fb[b_][0:P, :], in1=x5[0:P, xi, :], op=ALU.add),
                             reads=[B_fb[banks[k_]], Bx], writes=[Bx])
                        S.dma("sp", lambda e, xi=xi, r0=r0, P=P, ng=ng: e.dma_start(out=xdst[r0:r0 + P, ng * 512:(ng + 1) * 512],
                                                                                in_=x5[0:P, xi, :]),
                              reads=[Bx], writes=[bf("xdst_any")])
        S.barrier()
        return True

    prep_weights(0)
    src = xin
    for L in range(NL):
        dst = y if L == NL - 1 else xs[1 + (L % 2)]
        ok = layer(L, src, xs[0], dst)
        if not ok:
            break
        src = dst
    S.barrier()
    S.run()
    S.close()
    es.close()
    return nc, S


_CACHE = {}


def _host_consts(rel_table):
    k = np.arange(128)[:, None]
    q = np.arange(128)[None, :]
    na = rel_table.shape[0]
    biasA = np.zeros((2, H, 128, 640), np.float32)
    maskA = np.ones((128, 640), np.float32)
    for o in range(5):
        d = np.clip(128 * o + q - k, -128, 128) + 128
        biasA[:na, :, :, o * 128:(o + 1) * 128] = rel_table[:, :, d]
    maskA[:, 0:128] = np.where((q < 64) & (k >= 64), 0.0, 1.0)
    maskA[:, 512:640] = np.where((q >= 64) & (k < 64), 0.0, 1.0)
    biasS = np.zeros((2, H, 128, 80), np.float32)
    qi = np.arange(16)[None, :]
    for c in range(5):
        kpos = 128 * c + np.arange(128)[:, None]
        d = np.clip(512 + qi - kpos, -128, 128) + 128
        biasS[:na, :, :, c * 16:(c + 1) * 16] = rel_table[:, :, d]
    maskS = np.where(np.arange(16)[:, None] <= np.arange(16)[None, :], 0.0, NEGM).astype(np.float32)
    return biasA, maskA, biasS, maskS


def _mask_d():
    m = np.zeros((128, 4, 512), np.float32)
    ki = np.arange(128)[:, None]
    qi = np.arange(512)[None, :]
    for u in range(4):
        m[:, u, :] = np.where(128 * u + ki <= qi, 0.0, NEGM)
    return m.reshape(128, 4 * 512).astype(ml_dtypes.bfloat16)


def _tabs(p):
    t = np.zeros((128, 24), np.float32)
    for r in range(8):
        t[:, r] = 0.0 if r <= p else NEGM
        t[:, 8 + r] = 1.0 if r == p else 0.0
        if r < 7:
            t[:, 16 + r] = 0.0 if r == p - 1 else NEGM
        else:
            t[:, 16 + r] = 0.0 if p == 0 else NEGM
    return t


def kernel(x_prompt, x_sample, cache_a_k, cache_a_v, cache_b_k, cache_b_v, cache_b_logf,
           g_attn, w_qkv, g_q, g_k, w_o, rel_table, w_f, b_f, g_ffn, w_gate, w_up, w_down, _NL=DEPTH, _stop=None):
    f = lambda a: np.ascontiguousarray(np.asarray(a, dtype=np.float32))
    x_prompt, x_sample = f(x_prompt), f(x_sample)
    cache_a_k, cache_a_v, cache_b_k, cache_b_v, cache_b_logf = map(f, (cache_a_k, cache_a_v, cache_b_k, cache_b_v, cache_b_logf))
    g_attn, w_qkv, g_q, g_k, w_o, rel_table, w_f, b_f, g_ffn, w_gate, w_up, w_down = map(
        f, (g_attn, w_qkv, g_q, g_k, w_o, rel_table, w_f, b_f, g_ffn, w_gate, w_up, w_down))
    NL = _NL
    key = (NL, _stop)
    if key not in _CACHE:
        _CACHE[key] = build(NL, _stop)[0]
    nc = _CACHE[key]
    biasA, maskA, biasS, maskS = _host_consts(rel_table)
    in_maps = []
    maskD = _mask_d()
    for c in range(8):
        xin = np.concatenate([x_prompt[bb, 512 * (c + 8 * j):512 * (c + 8 * j) + 512] for bb in range(2) for j in range(2)]
                             + [x_sample[2 * c], x_sample[2 * c + 1]], axis=0)
        oh = np.zeros((16, 8), np.float32)
        oh[:, c] = 1.0
        rs = slice(256 * c, 256 * (c + 1))
        in_maps.append({
            "xin": np.ascontiguousarray(xin),
            "cak": np.ascontiguousarray(cache_a_k[:, 2 * c:2 * c + 2].reshape(2, 2, LA, D)),
            "cav": np.ascontiguousarray(cache_a_v[:, 2 * c:2 * c + 2].reshape(2, 2, LA, D)),
            "cbk": np.ascontiguousarray(cache_b_k[:, 2 * c:2 * c + 2].reshape(2, 2, LB, D)),
            "cbv": np.ascontiguousarray(cache_b_v[:, 2 * c:2 * c + 2].reshape(2, 2, LB, D)),
            "cblf": np.ascontiguousarray(cache_b_logf[:, 2 * c:2 * c + 2]),
            "g_attn": g_attn, "g_ffn": g_ffn, "g_q": g_q, "g_k": g_k, "b_f": b_f,
            "wqkv_s": np.ascontiguousarray(w_qkv[:, rs]), "wo_s": np.ascontiguousarray(w_o[:, rs]),
            "wg_s": np.ascontiguousarray(w_gate[:, rs]), "wu_s": np.ascontiguousarray(w_up[:, rs]),
            "wd_s": np.ascontiguousarray(w_down[:, 704 * c:704 * (c + 1)]), "wf_s": np.ascontiguousarray(w_f[:, rs]),
            "biasA": biasA, "maskA": maskA, "biasS": biasS, "maskD": maskD, "maskS": maskS,
            "tabs": _tabs(c), "onehot": oh,
        })
    res = run_bass_kernel_spmd(nc, in_maps, core_ids=list(range(8)))
    R = res.results
    y_prompt = np.zeros((2, 8192, D), np.float32)
    y_sample = np.zeros((16, 16, D), np.float32)
    a_kp = np.zeros((2, 2, 512, H, HD), np.float32); a_vp = np.zeros_like(a_kp)
    b_kp = np.zeros((2, 2, 8192, H, HD), np.float32); b_vp = np.zeros_like(b_kp)
    b_fp = np.zeros((2, 2, 8192, H), np.float32)
    a_ks = np.zeros((2, 16, 16, H, HD), np.float32); a_vs = np.zeros_like(a_ks)
    b_ks = np.zeros((2, 16, 16, H, HD), np.float32); b_vs = np.zeros_like(b_ks)
    b_fs = np.zeros((2, 16, 16, H), np.float32)
    for c in range(8):
        r = R[c]
        yy = np.asarray(r["y"]).reshape(NTOK, D)
        kk = np.asarray(r["ko"]).reshape(DEPTH, NTOK, D)
        vv = np.asarray(r["vo"]).reshape(DEPTH, NTOK, D)
        lf = np.asarray(r["lfo"]).reshape(2, NTOK, H)
        for bb in range(2):
            for j in range(2):
                g = c + 8 * j
                lg = 2 * bb + j
                y_prompt[bb, 512 * g:512 * g + 512] = yy[512 * lg:512 * lg + 512]
                for ib in range(2):
                    b_kp[ib, bb, 512 * g:512 * g + 512] = kk[2 * ib + 1, 512 * lg:512 * lg + 512].reshape(512, H, HD)
                    b_vp[ib, bb, 512 * g:512 * g + 512] = vv[2 * ib + 1, 512 * lg:512 * lg + 512].reshape(512, H, HD)
                    b_fp[ib, bb, 512 * g:512 * g + 512] = lf[ib, 512 * lg:512 * lg + 512]
            if c == 7:
                lg = 2 * bb + 1
                for ia in range(2):
                    a_kp[ia, bb] = kk[2 * ia, 512 * lg:512 * lg + 512].reshape(512, H, HD)
                    a_vp[ia, bb] = vv[2 * ia, 512 * lg:512 * lg + 512].reshape(512, H, HD)
        for s in range(2):
            rows = slice(NPR + 16 * s, NPR + 16 * s + 16)
            y_sample[2 * c + s] = yy[rows]
            for i2 in range(2):
                a_ks[i2, 2 * c + s] = kk[2 * i2, rows].reshape(16, H, HD)
                a_vs[i2, 2 * c + s] = vv[2 * i2, rows].reshape(16, H, HD)
                b_ks[i2, 2 * c + s] = kk[2 * i2 + 1, rows].reshape(16, H, HD)
                b_vs[i2, 2 * c + s] = vv[2 * i2 + 1, rows].reshape(16, H, HD)
                b_fs[i2, 2 * c + s] = lf[i2, rows]
    return (y_prompt, y_sample, a_kp, a_vp, b_kp, b_vp, b_fp, a_ks, a_vs, b_ks, b_vs, b_fs)
```
